# Optimizing a Trainium2 kernel written in Bass

```python
import jax, jax.numpy as jnp
from jax import lax
import numpy as np

D_MODEL = 4096
BATCH = 2
SEQ = 8192
DEPTH = 1

CHUNK = 64
Q_BLOCK = 128
SB_HEADS = 16
SB_HEAD_DIM = 128
SB_WIDTH = SB_HEADS * SB_HEAD_DIM
RET_HEADS = 8
RET_HEAD_DIM = 256
RET_WIDTH = RET_HEADS * RET_HEAD_DIM
MIX_WIDTH = SB_WIDTH + RET_WIDTH
IN_WIDTH = 3 * SB_WIDTH + 4 * RET_WIDTH
D_FF = 4 * D_MODEL
ROPE_BASE = 10000.0
EPS = 1e-6

kernel_name = "hymba_stickbreak_retention_block"


def rms_norm(x, g):
    xf = x.astype(jnp.float32)
    y = xf * lax.rsqrt(jnp.mean(xf * xf, axis=-1, keepdims=True) + EPS)
    return (y * g.astype(jnp.float32)).astype(x.dtype)


def to_heads(t, n_heads):
    b, s, w = t.shape
    return t.reshape(b, s, n_heads, w // n_heads).transpose(0, 2, 1, 3)


def from_heads(t):
    b, h, s, d = t.shape
    return t.transpose(0, 2, 1, 3).reshape(b, s, h * d)


def stick_breaking_attention(q, k, v):
    b, h, s, dh = q.shape
    scale = dh ** -0.5
    outs = []
    for start in range(0, s, Q_BLOCK):
        end = start + Q_BLOCK
        qb = q[:, :, start:end]
        kp = k[:, :, :end]
        vp = v[:, :, :end]
        z = jnp.einsum('bhqd,bhkd->bhqk', qb, kp).astype(jnp.float32) * scale
        t_pos = start + jnp.arange(Q_BLOCK)[:, None]
        s_pos = jnp.arange(end)[None, :]
        mask = s_pos < t_pos
        log_beta = jax.nn.log_sigmoid(z)
        log_one_minus = jnp.where(mask, log_beta - z, 0.0)
        tail = lax.cumsum(log_one_minus, axis=3, reverse=True) - log_one_minus
        a = jnp.where(mask, jnp.exp(log_beta + tail), 0.0)
        outs.append(jnp.einsum('bhqk,bhkd->bhqd', a.astype(vp.dtype), vp))
    return jnp.concatenate(outs, axis=2)


def rotary(x, pos):
    half = x.shape[-1] // 2
    inv_freq = ROPE_BASE ** (-jnp.arange(half, dtype=jnp.float32) / half)
    ang = pos.astype(jnp.float32)[:, None] * inv_freq[None, :]
    cos, sin = jnp.cos(ang), jnp.sin(ang)
    x1, x2 = x[..., :half], x[..., half:]
    return jnp.concatenate([x1 * cos - x2 * sin, x2 * cos + x1 * sin], axis=-1)


def retention_chunkwise(q, k, v):
    b, h, s, dk = q.shape
    dv = v.shape[-1]
    n = s // CHUNK
    log_gamma = jnp.log1p(-(2.0 ** (-5.0 - jnp.arange(h, dtype=jnp.float32))))
    idx = jnp.arange(CHUNK, dtype=jnp.float32)
    rel = idx[:, None] - idx[None, :]
    decay_mask = jnp.where(rel >= 0.0,
                           jnp.exp(log_gamma[:, None, None] * jnp.maximum(rel, 0.0)), 0.0)
    q_decay = jnp.exp(log_gamma[:, None] * (idx + 1.0))
    k_decay = jnp.exp(log_gamma[:, None] * (CHUNK - 1.0 - idx))
    chunk_decay = jnp.exp(log_gamma * CHUNK)

    qc = q.reshape(b, h, n, CHUNK, dk)
    kc = k.reshape(b, h, n, CHUNK, dk)
    vc = v.reshape(b, h, n, CHUNK, dv)
    scores = jnp.einsum('bhncd,bhnmd->bhncm', qc, kc) * decay_mask[None, :, None]
    inner = jnp.einsum('bhncm,bhnme->bhnce', scores, vc)

    def step(state, xs):
        q_i, k_i, v_i = xs
        cross = jnp.einsum('bhcd,bhde->bhce', q_i, state) * q_decay[None, :, :, None]
        state = state * chunk_decay[None, :, None, None] + jnp.einsum(
            'bhcd,bhce->bhde', k_i * k_decay[None, :, :, None], v_i)
        return state, cross

    xs = (jnp.moveaxis(qc, 2, 0), jnp.moveaxis(kc, 2, 0), jnp.moveaxis(vc, 2, 0))
    state0 = jnp.zeros((b, h, dk, dv), jnp.float32)
    _, cross = lax.scan(step, state0, xs)
    cross = jnp.moveaxis(cross, 0, 2)
    return (inner + cross).reshape(b, h, s, dv)


def head_group_norm(y, g):
    mu = jnp.mean(y, axis=-1, keepdims=True)
    yc = y - mu
    yn = yc * lax.rsqrt(jnp.mean(yc * yc, axis=-1, keepdims=True) + EPS)
    return from_heads(yn) * g.astype(jnp.float32)


def setup_inputs(seed: int = 0) -> dict:
    key = jax.random.key(seed)
    ks = jax.random.split(key, 10)
    f32 = jnp.float32

    def gain(k, n):
        return 1.0 + 0.02 * jax.random.normal(k, (DEPTH, n), f32)

    return {
        "x": jax.random.normal(ks[0], (BATCH, SEQ, D_MODEL), f32),
        "attn_norm_g": gain(ks[1], D_MODEL),
        "w_in": jax.random.normal(ks[2], (DEPTH, D_MODEL, IN_WIDTH), f32) * D_MODEL ** -0.5,
        "sb_norm_g": gain(ks[3], SB_WIDTH),
        "ret_norm_g": gain(ks[4], RET_WIDTH),
        "w_out": jax.random.normal(ks[5], (DEPTH, MIX_WIDTH, D_MODEL), f32) * MIX_WIDTH ** -0.5,
        "mlp_norm_g": gain(ks[6], D_MODEL),
        "w_up": jax.random.normal(ks[7], (DEPTH, D_MODEL, D_FF), f32) * D_MODEL ** -0.5,
        "w_down": jax.random.normal(ks[8], (DEPTH, D_FF, D_MODEL), f32) * D_FF ** -0.5,
        "final_norm_g": 1.0 + 0.02 * jax.random.normal(ks[9], (D_MODEL,), f32),
    }


def reference(x, attn_norm_g, w_in, sb_norm_g, ret_norm_g, w_out, mlp_norm_g, w_up, w_down,
              final_norm_g):
    s = x.shape[1]
    pos = jnp.arange(s, dtype=jnp.int32)
    for l in range(DEPTH):
        h = rms_norm(x, attn_norm_g[l])
        proj = h @ w_in[l]
        o = 0
        sb_q = proj[..., o:o + SB_WIDTH]; o += SB_WIDTH
        sb_k = proj[..., o:o + SB_WIDTH]; o += SB_WIDTH
        sb_v = proj[..., o:o + SB_WIDTH]; o += SB_WIDTH
        r_q = proj[..., o:o + RET_WIDTH]; o += RET_WIDTH
        r_k = proj[..., o:o + RET_WIDTH]; o += RET_WIDTH
        r_v = proj[..., o:o + RET_WIDTH]; o += RET_WIDTH
        r_g = proj[..., o:o + RET_WIDTH]

        sb = stick_breaking_attention(to_heads(sb_q, SB_HEADS), to_heads(sb_k, SB_HEADS),
                                      to_heads(sb_v, SB_HEADS))
        sb = rms_norm(from_heads(sb), sb_norm_g[l])

        rq = rotary(to_heads(r_q, RET_HEADS).astype(jnp.float32), pos)
        rk = rotary(to_heads(r_k, RET_HEADS).astype(jnp.float32), pos) * RET_HEAD_DIM ** -0.5
        rv = to_heads(r_v, RET_HEADS).astype(jnp.float32)
        ret = head_group_norm(retention_chunkwise(rq, rk, rv), ret_norm_g[l])
        ret = (jax.nn.silu(r_g.astype(jnp.float32)) * ret).astype(x.dtype)

        mixed = jnp.concatenate([sb.astype(x.dtype), ret], axis=-1)
        x = x + mixed @ w_out[l]

        h2 = rms_norm(x, mlp_norm_g[l])
        x = x + jnp.square(jax.nn.relu(h2 @ w_up[l])) @ w_down[l]
    return rms_norm(x, final_norm_g)
```

```python
from contextlib import ExitStack

import ml_dtypes
import numpy as np

import concourse.bass as bass
import concourse.mybir as mybir
from concourse.bass_utils import run_bass_kernel_spmd

F32 = mybir.dt.float32
BF16 = mybir.dt.bfloat16
AF = mybir.ActivationFunctionType
ALU = mybir.AluOpType
AX = mybir.AxisListType

D = 4096
S_FULL = 8192
DFF = 16384
EPS = 1e-6
NDQ = 16
NDS = 3 * NDQ


class T:
    __slots__ = ("w", "r")

    def __init__(self):
        self.w = {}
        self.r = {}


class Sched:
    def __init__(self, nc, ctx):
        self.nc = nc
        self.eng = {"pe": nc.tensor, "act": nc.scalar, "dve": nc.vector,
                    "pool": nc.gpsimd, "sp": nc.sync}
        self.sem = {k: ctx.enter_context(nc.semaphore("sem_" + k))
                    for k in ("pe", "act", "dve", "pool")}
        self.cnt = {k: 0 for k in self.sem}
        self.dsem = [ctx.enter_context(nc.semaphore("dsem%d" % i)) for i in range(NDS)]
        self.dcnt = [0] * NDS
        self.qbase = {"sp": 0, "act": NDQ, "pool": 2 * NDQ}
        self.qnext = {"sp": 0, "act": 0, "pool": 0}
        self.seen = {k: {} for k in self.eng}

    def _wait(self, e, deps):
        for k, v in deps.items():
            if k == e and e == "pe":
                continue
            if self.seen[e].get(k, 0) >= v:
                continue
            sem = self.sem[k] if isinstance(k, str) else self.dsem[k]
            self.eng[e].wait_ge(sem, v)
            self.seen[e][k] = v

    @staticmethod
    def _merge(d, src):
        for k, v in src.items():
            if d.get(k, 0) < v:
                d[k] = v

    def _deps(self, reads, writes, partial):
        deps = {}
        for t in reads:
            self._merge(deps, t.w)
        for t in writes:
            self._merge(deps, t.r)
            if not partial:
                self._merge(deps, t.w)
        return deps

    def _record(self, tok, reads, writes, partial):
        k, v = tok
        for t in reads:
            t.r[k] = v
        for t in writes:
            if partial:
                t.w[k] = v
            else:
                t.w = {k: v}
                t.r = {}

    def op(self, e, fn, reads=(), writes=(), partial=False):
        self._wait(e, self._deps(reads, writes, partial))
        ins = fn(self.eng[e])
        self.cnt[e] += 1
        ins.then_inc(self.sem[e], 1)
        self._record((e, self.cnt[e]), reads, writes, partial)

    def dma(self, q, out, in_, reads=(), writes=(), partial=False):
        base = self.qbase[q]
        i = base + self.qnext[q]
        self.qnext[q] = (self.qnext[q] + 1) % NDQ
        deps = self._deps(reads, writes, partial)
        if self.dcnt[i] > 0:
            deps[i] = max(deps.get(i, 0), self.dcnt[i])
        self._wait(q, deps)
        self.dcnt[i] += 16
        self.eng[q].dma_start(out=out, in_=in_).then_inc(self.dsem[i], 16)
        self._record((i, self.dcnt[i]), reads, writes, partial)

    def barrier(self, engines=("pe", "act", "dve", "pool", "sp")):
        deps = {k: v for k, v in self.cnt.items() if v > 0}
        for i in range(NDS):
            if self.dcnt[i] > 0:
                deps[i] = self.dcnt[i]
        for e in engines:
            self._wait(e, dict(deps))


class Rot:
    def __init__(self, items):
        self.items = items
        self.i = 0

    def next(self):
        it = self.items[self.i]
        self.i = (self.i + 1) % len(self.items)
        return it


def _alloc(nc, ctx, name, shape, dt, n=1, psum=False):
    out = []
    for i in range(n):
        f = nc.psum_tensor if psum else nc.sbuf_tensor
        t = ctx.enter_context(f("%s%d" % (name, i), shape, dt))
        out.append((t, T()))
    return out


def _stats(S, ones, src_chunks, src_t, sqrot, bank, bank_t, rt, rt_t, eps_t, inv_n, n):
    nk = len(src_chunks)
    for k, src in enumerate(src_chunks):
        sq, sq_t = sqrot.next()
        S.op("act", lambda e, sq=sq, src=src: e.activation(out=sq[:, 0:n], in_=src, func=AF.Square),
             reads=[src_t[k] if isinstance(src_t, list) else src_t], writes=[sq_t])
        S.op("pe", lambda e, sq=sq, k=k: e.matmul(bank[:, 0:n], lhsT=ones[0][:, :], rhs=sq[:, 0:n],
                                                   start=(k == 0), stop=(k == nk - 1)),
             reads=[sq_t, ones[1]], writes=[bank_t], partial=(k > 0))
    S.op("act", lambda e: e.activation(out=rt[:, 0:n], in_=bank[:, 0:n], func=AF.Sqrt,
                                       bias=eps_t[0][:, 0:1], scale=inv_n),
         reads=[bank_t, eps_t[1]], writes=[rt_t])
    S.op("dve", lambda e: e.reciprocal(out=rt[:, 0:n], in_=rt[:, 0:n]), reads=[rt_t], writes=[rt_t])


def build_fused(NP=6144, NO=2048, HG=4, do_ffn=True, dbg=False):
    TB = 512
    W = NP + NO
    nch = W // 128
    npch = NP // 128
    NSB = 4 * HG
    NRT = 2 * HG
    nc = bass.Bass("TRN2", target_bir_lowering=False)
    dt_in = lambda name, shape, dt=F32: nc.dram_tensor(name, shape, dt, kind="ExternalInput").ap()
    xT = dt_in("xT", [D, W])
    win_p = dt_in("win_p", [7 * HG * 128, 32 * 512])
    g_attn = dt_in("g_attn", [128, 32])
    cosT = dt_in("cosT", [128, W])
    sinT = dt_in("sinT", [128, W])
    cbf = dt_in("cbf", [128, 5 * 128], BF16)
    mask01 = dt_in("mask01", [128, 128])
    kbias = dt_in("kbias", [128, nch])
    rmask = dt_in("rmask", [128, NRT * 128])
    rvec = dt_in("rvec", [128, 13 * NRT])
    gret = dt_in("gret", [128, NRT * 256])
    win_b = nc.dram_tensor("win_b", [7 * HG * 128, 32 * 512], BF16, kind="Internal").ap()
    QT = nc.dram_tensor("QT", [NSB * 128, NO], BF16, kind="Internal").ap()
    KT = nc.dram_tensor("KT", [NSB * 128, W], BF16, kind="Internal").ap()
    V = nc.dram_tensor("V", [W, NSB * 128], BF16, kind="Internal").ap()
    RQT = nc.dram_tensor("RQT", [NRT * 256, NO], BF16, kind="Internal").ap()
    RKT = nc.dram_tensor("RKT", [NRT * 256, W], BF16, kind="Internal").ap()
    RV = nc.dram_tensor("RV", [W, NRT * 256], BF16, kind="Internal").ap()
    G = nc.dram_tensor("G", [NO, NRT * 256], F32, kind="Internal").ap()
    mixT = nc.dram_tensor("mixT", [D, NO], BF16, kind=("ExternalOutput" if dbg else "Internal")).ap()
    t_winb = [T() for _ in range(7 * HG)]
    t_QT, t_KT, t_V, t_RQT, t_RKT, t_RV, t_G, t_mix = (T() for _ in range(8))
    t_out = T()
    if do_ffn:
        wo_p = dt_in("wo_p", [16 * 128, 8192])
        wu_p = dt_in("wu_p", [64 * 128, 8192])
        wd_p = dt_in("wd_p", [64 * 128, 8192])
        gvec = dt_in("gvec", [128, 80])
        outT = nc.dram_tensor("outT", [D, NO], F32, kind="ExternalOutput").ap()
        wo_b = nc.dram_tensor("wo_b", [16 * 128, 8192], BF16, kind="Internal").ap()
        wu_b = nc.dram_tensor("wu_b", [64 * 128, 8192], BF16, kind="Internal").ap()
        wd_b = nc.dram_tensor("wd_b", [64 * 128, 8192], BF16, kind="Internal").ap()
        t_wo = [T() for _ in range(16)]
        t_wu = [T() for _ in range(64)]
        t_wd = [T() for _ in range(64)]

    with ExitStack() as ctx:
        S = Sched(nc, ctx)
        cb = _alloc(nc, ctx, "cbf_s", [128, 640], BF16)[0]
        eps_t = _alloc(nc, ctx, "eps_s", [128, 1], F32)[0]
        one_t = _alloc(nc, ctx, "one_s", [128, 1], F32)[0]
        S.dma("sp", cb[0][:, :], cbf[:, :], writes=[cb[1]])
        S.op("pool", lambda e: e.memset(eps_t[0][:, :], EPS), writes=[eps_t[1]])
        S.op("pool", lambda e: e.memset(one_t[0][:, :], 1.0), writes=[one_t[1]])
        ones = (cb[0][:, 0:128], cb[1])
        ident = (cb[0][:, 128:256], cb[1])
        negU = (cb[0][:, 256:384], cb[1])
        negOnes = (cb[0][:, 384:512], cb[1])
        zerosb = (cb[0][:, 512:640], cb[1])
        prefix_types = (1, 2, 4, 5)
        for ty in (1, 2, 4, 5, 0, 3, 6):
            for hg in range(HG):
                gi = ty * HG + hg
                S.dma("pool", win_b[gi * 128:(gi + 1) * 128, :], win_p[gi * 128:(gi + 1) * 128, :], writes=[t_winb[gi]])

        with ExitStack() as pa:
            banks = _alloc(nc, pa, "bankA", [128, 512], F32, n=8, psum=True)
            gat = _alloc(nc, pa, "gat_s", [128, 32], F32)[0]
            grt = _alloc(nc, pa, "grt_s", [128, NRT * 256], F32)[0]
            S.dma("sp", gat[0][:, :], g_attn[:, :], writes=[gat[1]])
            S.dma("sp", grt[0][:, :], gret[:, :], writes=[grt[1]])
            xs = Rot(_alloc(nc, pa, "xs", [128, 4, TB], F32, n=3))
            hTs = Rot(_alloc(nc, pa, "hT", [128, 32, TB], BF16, n=2))
            pans = Rot(_alloc(nc, pa, "pan", [128, 32, 512], BF16, n=2))
            sqrot = Rot(_alloc(nc, pa, "sq", [128, TB], BF16, n=2))
            rts = Rot(_alloc(nc, pa, "rt", [128, TB], F32, n=4))
            stg = Rot(_alloc(nc, pa, "stg", [128, 512], BF16, n=4))
            stg32 = Rot(_alloc(nc, pa, "stg32", [128, 512], F32, n=2))
            tmp32 = Rot(_alloc(nc, pa, "tmp32", [128, 512], F32, n=4))
            cs = Rot(_alloc(nc, pa, "cs", [128, 2, TB], F32, n=3))
            mmb = Rot(banks[0:6])
            stb = Rot(banks[6:8])
            xTv = xT.rearrange("(k p) t -> p k t", p=128)
            evq = [0]

            def evac_copy(bank, bank_t, dst, dst_t):
                S.op("dve", lambda e: e.tensor_copy(out=dst, in_=bank), reads=[bank_t], writes=[dst_t])

            def stats_gen(t0, holder):
                sbk, sbk_t = stb.next()
                rt, rt_t = rts.next()
                first = True
                for kg in range(8):
                    xb, xb_t = xs.next()
                    S.dma("act", xb[:, :, :], xTv[:, kg * 4:(kg + 1) * 4, t0:t0 + TB], writes=[xb_t])
                    for kk in range(4):
                        sq, sq_t = sqrot.next()
                        S.op("act", lambda e, sq=sq, xb=xb, kk=kk: e.activation(out=sq[:, :], in_=xb[:, kk, :], func=AF.Square),
                             reads=[xb_t], writes=[sq_t])
                        last = (kg == 7 and kk == 3)
                        S.op("pe", lambda e, sq=sq, first=first, last=last: e.matmul(sbk[:, :], lhsT=ones[0], rhs=sq[:, :], start=first, stop=last),
                             reads=[sq_t, ones[1]], writes=[sbk_t], partial=(not first))
                        first = False
                        if not last:
                            yield
                S.op("act", lambda e: e.activation(out=rt[:, :], in_=sbk[:, :], func=AF.Sqrt, bias=eps_t[0][:, 0:1], scale=1.0 / D),
                     reads=[sbk_t, eps_t[1]], writes=[rt_t])
                S.op("dve", lambda e: e.reciprocal(out=rt[:, :], in_=rt[:, :]), reads=[rt_t], writes=[rt_t])
                holder.append((rt, rt_t))

            def make_stats(t0):
                holder = []
                for _ in stats_gen(t0, holder):
                    pass
                return holder[0]

            pending = []

            def tick():
                while pending:
                    try:
                        next(pending[0])
                        return
                    except StopIteration:
                        pending.pop(0)

            def make_norm(t0, rt, rt_t):
                hT, hT_t = hTs.next()
                for kg in range(8):
                    xb, xb_t = xs.next()
                    S.dma("act", xb[:, :, :], xTv[:, kg * 4:(kg + 1) * 4, t0:t0 + TB], writes=[xb_t])
                    for kk in range(4):
                        k = kg * 4 + kk
                        S.op("dve", lambda e, xb=xb, kk=kk, k=k: e.scalar_tensor_tensor(
                            out=hT[:, k, :], in0=xb[:, kk, :], scalar=gat[0][:, k:k + 1], in1=rt[:, :],
                            op0=ALU.mult, op1=ALU.mult),
                            reads=[xb_t, gat[1], rt_t], writes=[hT_t], partial=(k > 0))
                cst, cst_t = cs.next()
                S.dma("sp", cst[:, 0, :], cosT[:, t0:t0 + TB], writes=[cst_t])
                S.dma("sp", cst[:, 1, :], sinT[:, t0:t0 + TB], writes=[cst_t], partial=True)
                return (hT, hT_t, cst, cst_t)

            def project(ty, hg, pan, pan_t, hT, hT_t, cst, cst_t, t0):
                own = t0 >= NP
                to = t0 - NP
                if ty in (0, 1, 3, 4):
                    got = []
                    for ct in range(4):
                        bk, bk_t = mmb.next()
                        for k in range(32):
                            S.op("pe", lambda e, bk=bk, k=k, ct=ct: e.matmul(
                                bk[:, :], lhsT=pan[:, k, ct * 128:(ct + 1) * 128], rhs=hT[:, k, :],
                                start=(k == 0), stop=(k == 31)),
                                reads=[pan_t, hT_t], writes=[bk_t], partial=(k > 0))
                        tick()
                        if ty in (0, 1):
                            st, st_t = stg.next()
                            evac_copy(bk[:, :], bk_t, st[:, :], st_t)
                            r0 = (hg * 4 + ct) * 128
                            if ty == 0:
                                S.dma("pool", QT[r0:r0 + 128, to:to + TB], st[:, :], reads=[st_t], writes=[t_QT], partial=True)
                            else:
                                S.dma("pool", KT[r0:r0 + 128, t0:t0 + TB], st[:, :], reads=[st_t], writes=[t_KT], partial=True)
                        else:
                            got.append((bk, bk_t))
                            if ct % 2 == 1:
                                (b1, b1_t), (b2, b2_t) = got
                                got = []
                                sc = 1.0 if ty == 3 else 1.0 / 16.0
                                tms = [tmp32.next() for _ in range(4)]
                                srcs = [(b1, b1_t, 0), (b2, b2_t, 1), (b2, b2_t, 0), (b1, b1_t, 1)]
                                for (tm, tm_t), (bb, bb_t, ci) in zip(tms, srcs):
                                    S.op("dve", lambda e, tm=tm, bb=bb, ci=ci: e.scalar_tensor_tensor(
                                        out=tm[:, :], in0=bb[:, :], scalar=sc, in1=cst[:, ci, :],
                                        op0=ALU.mult, op1=ALU.mult),
                                        reads=[bb_t, cst_t], writes=[tm_t])
                                hh = ct // 2
                                for half, (ia, ib, opx) in enumerate(((0, 1, ALU.subtract), (2, 3, ALU.add))):
                                    st, st_t = stg.next()
                                    S.op("pool", lambda e, st=st, ia=ia, ib=ib, opx=opx: e.tensor_tensor(
                                        out=st[:, :], in0=tms[ia][0][:, :], in1=tms[ib][0][:, :], op=opx),
                                        reads=[tms[ia][1], tms[ib][1]], writes=[st_t])
                                    r0 = ((hg * 2 + hh) * 2 + half) * 128
                                    if ty == 3:
                                        S.dma("pool", RQT[r0:r0 + 128, to:to + TB], st[:, :], reads=[st_t], writes=[t_RQT], partial=True)
                                    else:
                                        S.dma("pool", RKT[r0:r0 + 128, t0:t0 + TB], st[:, :], reads=[st_t], writes=[t_RKT], partial=True)
                else:
                    for tt in range(4):
                        bk, bk_t = mmb.next()
                        for k in range(32):
                            S.op("pe", lambda e, bk=bk, k=k, tt=tt: e.matmul(
                                bk[:, :], lhsT=hT[:, k, tt * 128:(tt + 1) * 128], rhs=pan[:, k, :],
                                start=(k == 0), stop=(k == 31)),
                                reads=[pan_t, hT_t], writes=[bk_t], partial=(k > 0))
                        tick()
                        r0 = t0 + tt * 128
                        if ty in (2, 5):
                            st, st_t = stg.next()
                            evac_copy(bk[:, :], bk_t, st[:, :], st_t)
                            dstT, dst_t = (V, t_V) if ty == 2 else (RV, t_RV)
                            S.dma("pool", dstT[r0:r0 + 128, hg * 512:(hg + 1) * 512], st[:, :], reads=[st_t], writes=[dst_t], partial=True)
                        else:
                            tm, tm_t = tmp32.next()
                            S.op("act", lambda e, tm=tm, bk=bk: e.activation(out=tm[:, :], in_=bk[:, :], func=AF.Silu),
                                 reads=[bk_t], writes=[tm_t])
                            st, st_t = stg32.next()
                            S.op("pool", lambda e, st=st, tm=tm: e.tensor_tensor(out=st[:, :], in0=tm[:, :], in1=grt[0][:, hg * 512:(hg + 1) * 512], op=ALU.mult),
                                 reads=[tm_t, grt[1]], writes=[st_t])
                            ro = r0 - NP
                            S.dma("pool", G[ro:ro + 128, hg * 512:(hg + 1) * 512], st[:, :], reads=[st_t], writes=[t_G], partial=True)

            nsb = W // 1024
            jobs = []
            for sblk in range(nsb):
                own_ = sblk * 1024 >= NP
                types = (1, 2, 4, 5, 0, 3, 6) if own_ else prefix_types
                jobs.append([(ty, hg) for ty in types for hg in range(HG)])
            t0s = lambda sblk: [sblk * 1024 + sub * TB for sub in range(2)]
            cur = [make_norm(t0, *make_stats(t0)) for t0 in t0s(0)]
            nxt_rt = None
            for sblk in range(nsb):
                jl = jobs[sblk]
                nxt = [None, None]
                for ji, (ty, hg) in enumerate(jl):
                    gi = ty * HG + hg
                    pan, pan_t = pans.next()
                    S.dma("sp", pan[:, :, :], win_b[gi * 128:(gi + 1) * 128, :].rearrange("p (k c) -> p k c", k=32),
                          reads=[t_winb[gi]], writes=[pan_t])
                    if sblk + 1 < nsb and ji == 1:
                        nxt_rt = [[], []]
                        for sub, t0n in enumerate(t0s(sblk + 1)):
                            pending.append(stats_gen(t0n, nxt_rt[sub]))
                    lastj = (ji == len(jl) - 1)
                    if lastj:
                        while pending:
                            tick()
                    for sub in range(2):
                        hT, hT_t, cst, cst_t = cur[sub]
                        project(ty, hg, pan, pan_t, hT, hT_t, cst, cst_t, t0s(sblk)[sub])
                        if lastj and sblk + 1 < nsb:
                            nxt[sub] = make_norm(t0s(sblk + 1)[sub], *nxt_rt[sub][0])
                cur = nxt
            S.barrier()

        cast_jobs = []
        if do_ffn:
            for pn in range(16):
                cast_jobs.append((wo_b[pn * 128:(pn + 1) * 128, :], wo_p[pn * 128:(pn + 1) * 128, :], t_wo[pn]))
            for fb in range(8):
                for i in range(8):
                    pn = fb * 8 + i
                    cast_jobs.append((wu_b[pn * 128:(pn + 1) * 128, :], wu_p[pn * 128:(pn + 1) * 128, :], t_wu[pn]))
                for i in range(8):
                    pn = fb * 8 + i
                    cast_jobs.append((wd_b[pn * 128:(pn + 1) * 128, :], wd_p[pn * 128:(pn + 1) * 128, :], t_wd[pn]))

        def issue_cast():
            if cast_jobs:
                dst, src, tr = cast_jobs.pop(0)
                S.dma("pool", dst, src, writes=[tr])

        with ExitStack() as pb:
            zb = Rot(_alloc(nc, pb, "bz", [128, 2, 512], F32, n=2, psum=True))
            wb = Rot(_alloc(nc, pb, "bw", [128, 2, 512], F32, n=1, psum=True))
            obanks = _alloc(nc, pb, "bo", [128, 512], F32, n=2, psum=True)
            m01 = _alloc(nc, pb, "m01_s", [128, 128], F32)[0]
            kb = _alloc(nc, pb, "kb_s", [128, nch], F32)[0]
            S.dma("sp", m01[0][:, :], mask01[:, :], writes=[m01[1]])
            S.dma("sp", kb[0][:, :], kbias[:, :], writes=[kb[1]])
            hd = Rot([tuple(_alloc(nc, pb, "hd%d_" % i + nm, shp, BF16)[0] for nm, shp in
                            (("q", [128, 2, NO]), ("k", [128, 2, W]), ("v", [128, 2, nch, 128])))
                      for i in range(2)])
            Es = Rot(_alloc(nc, pb, "E", [128, 2, 512], F32, n=4))
            SPs = Rot(_alloc(nc, pb, "SP", [128, 2, 512], BF16, n=4))
            Xs = Rot(_alloc(nc, pb, "X", [128, 2, 512], F32, n=3))
            ATs = Rot(_alloc(nc, pb, "AT", [128, 2, 512], BF16, n=3))
            Rbs = Rot(_alloc(nc, pb, "Rb", [128, 2, 512], BF16, n=4))
            ostg = Rot(_alloc(nc, pb, "ostg", [128, 512], BF16, n=4))
            scale = 128.0 ** -0.5
            its = []
            for hp in range(NSB // 2):
                for g in range(NO // 512):
                    for i in range(npch + 4 * g + 3, -1, -1):
                        its.append((hp, g, i))
            n_it = len(its)
            st8 = [None] * n_it
            hdata = {}
            Vv = V.rearrange("(n p) d -> p n d", p=128)

            def load_pair(hp):
                (q, q_t), (k, k_t), (v, v_t) = hd.next()
                for h in range(2):
                    hh = 2 * hp + h
                    S.dma("sp", q[:, h, :], QT[hh * 128:(hh + 1) * 128, :], reads=[t_QT], writes=[q_t], partial=(h > 0))
                    S.dma("sp", k[:, h, :], KT[hh * 128:(hh + 1) * 128, :], reads=[t_KT], writes=[k_t], partial=(h > 0))
                    for c0 in range(0, nch, 16):
                        c1 = min(nch, c0 + 16)
                        S.dma("sp", v[:, h, c0:c1, :], Vv[:, c0:c1, hh * 128:(hh + 1) * 128],
                              reads=[t_V], writes=[v_t], partial=(h > 0 or c0 > 0))
                hdata[hp] = ((q, q_t), (k, k_t), (v, v_t))

            def stage0(n):
                hp, g, i = its[n]
                if hp not in hdata:
                    load_pair(hp)
                (q, q_t), (k, k_t), (v, v_t) = hdata[hp]
                b = i - (npch + 4 * g)
                c0 = 128 * b if b >= 0 else 0
                d = dict(hp=hp, g=g, i=i, b=b, c0=c0, q=(q, q_t), k=(k, k_t), v=(v, v_t))
                st8[n] = d
                d["z"] = zb.next()
                z, z_t = d["z"]
                for h in range(2):
                    S.op("pe", lambda e, h=h: e.matmul(z[:, h, c0:512], lhsT=k[:, h, i * 128:(i + 1) * 128],
                                                       rhs=q[:, h, g * 512 + c0:(g + 1) * 512], start=True, stop=True),
                         reads=[q_t, k_t], writes=[z_t], partial=(h > 0))

            def stage1(n):
                d = st8[n]
                c0, i = d["c0"], d["i"]
                z, z_t = d["z"]
                d["E"] = Es.next()
                E, E_t = d["E"]
                S.op("act", lambda e: e.activation(out=E[:, :, c0:512], in_=z[:, :, c0:512], func=AF.Exp,
                                                   bias=kb[0][:, i:i + 1], scale=scale),
                     reads=[z_t, kb[1]], writes=[E_t])
                if d["b"] >= 0:
                    for h in range(2):
                        S.op("dve", lambda e, h=h: e.tensor_tensor(out=E[:, h, c0:c0 + 128], in0=E[:, h, c0:c0 + 128],
                                                                   in1=m01[0][:, :], op=ALU.mult),
                             reads=[E_t, m01[1]], writes=[E_t])

            def stage2a(n):
                d = st8[n]
                c0 = d["c0"]
                E, E_t = d["E"]
                d["SP"] = SPs.next()
                SP, SP_t = d["SP"]
                S.op("act", lambda e: e.activation(out=SP[:, :, c0:512], in_=E[:, :, c0:512], func=AF.Ln, bias=one_t[0][:, 0:1], scale=1.0),
                     reads=[E_t, one_t[1]], writes=[SP_t])

            def stage2b(n):
                d = st8[n]
                c0 = d["c0"]
                SP, SP_t = d["SP"]
                firstg = (d["i"] == npch + 4 * d["g"] + 3)
                d["W"] = wb.next()
                W_, W_t = d["W"]
                d["Rb"] = None if firstg else st8[n - 1]["Rbn"]
                for h in range(2):
                    S.op("pe", lambda e, h=h: e.matmul(W_[:, h, c0:512], lhsT=negU[0], rhs=SP[:, h, c0:512], start=True, stop=firstg),
                         reads=[SP_t, negU[1]], writes=[W_t], partial=(h > 0))
                    if not firstg:
                        Rb, Rb_t = d["Rb"]
                        S.op("pe", lambda e, h=h, Rb=Rb: e.matmul(W_[:, h, c0:512], lhsT=negOnes[0], rhs=Rb[:, h, c0:512], start=False, stop=True),
                             reads=[Rb_t, negOnes[1]], writes=[W_t], partial=True)
                if d["i"] > 0:
                    d["Rbn"] = Rbs.next()
                    Rn, Rn_t = d["Rbn"]
                    if firstg:
                        S.op("pool", lambda e: e.memset(Rn[:, :, 0:384], 0.0), writes=[Rn_t])
                        S.op("pool", lambda e: e.tensor_copy(out=Rn[:, :, c0:512], in_=SP[:, :, c0:512]),
                             reads=[SP_t], writes=[Rn_t], partial=True)
                    else:
                        Rb, Rb_t = d["Rb"]
                        if c0 > 0:
                            S.op("pool", lambda e: e.tensor_copy(out=Rn[:, :, 0:c0], in_=Rb[:, :, 0:c0]),
                                 reads=[Rb_t], writes=[Rn_t])
                        S.op("pool", lambda e: e.tensor_tensor(out=Rn[:, :, c0:512], in0=Rb[:, :, c0:512], in1=SP[:, :, c0:512], op=ALU.add),
                             reads=[Rb_t, SP_t], writes=[Rn_t], partial=(c0 > 0))

            def stage3a(n):
                d = st8[n]
                c0 = d["c0"]
                g, i, b = d["g"], d["i"], d["b"]
                W_, W_t = d["W"]
                E, E_t = d["E"]
                X, X_t = Xs.next()
                S.op("act", lambda e: e.activation(out=X[:, :, c0:512], in_=W_[:, :, c0:512], func=AF.Exp),
                     reads=[W_t], writes=[X_t])
                AT, AT_t = ATs.next()
                S.op("dve", lambda e: e.tensor_tensor(out=AT[:, :, c0:512], in0=E[:, :, c0:512], in1=X[:, :, c0:512], op=ALU.mult),
                     reads=[E_t, X_t], writes=[AT_t])
                d["AT"] = (AT, AT_t)

            def stage3b(n):
                d = st8[n]
                c0 = d["c0"]
                g, i, b = d["g"], d["i"], d["b"]
                AT, AT_t = d["AT"]
                firstg = (i == npch + 4 * g + 3)
                v, v_t = d["v"]
                q, q_t = d["q"]
                last = (i == 0)
                for h in range(2):
                    O, O_t = obanks[h]
                    if firstg:
                        S.op("pe", lambda e, O=O, h=h: e.matmul(O[:, :], lhsT=zerosb[0], rhs=q[:, h, 0:512], start=True, stop=False,
                                                                  skip_group_check=True),
                             reads=[q_t, zerosb[1]], writes=[O_t])
                    S.op("pe", lambda e, O=O, h=h: e.matmul(O[:, c0:512], lhsT=v[:, h, i, :], rhs=AT[:, h, c0:512], start=False, stop=last,
                                                              skip_group_check=True),
                         reads=[AT_t, v_t], writes=[O_t], partial=True)
                    if last:
                        st, st_t = ostg.next()
                        S.op("dve", lambda e, st=st, O=O: e.tensor_copy(out=st[:, :], in_=O[:, :]), reads=[O_t], writes=[st_t])
                        hh = 2 * d["hp"] + h
                        S.dma("sp", mixT[hh * 128:(hh + 1) * 128, g * 512:(g + 1) * 512], st[:, :],
                              reads=[st_t], writes=[t_mix], partial=True)
                if last and g == 0 and d["hp"] + 1 < NSB // 2 and (d["hp"] + 1) not in hdata:
                    load_pair(d["hp"] + 1)
                if n >= 2 and st8[n - 2] is not None:
                    st8[n - 2] = {kk: st8[n - 2][kk] for kk in ("Rbn",) if kk in st8[n - 2]}

            cast_every = max(1, (n_it - 40) // max(1, len(cast_jobs)))
            for t in range(n_it + 3):
                if t % cast_every == cast_every - 1:
                    issue_cast()
                if t == 0:
                    stage0(0)
                if t + 1 < n_it:
                    stage0(t + 1)
                if t < n_it:
                    stage1(t)
                if 0 <= t - 1 < n_it:
                    stage2a(t - 1)
                if 0 <= t - 2 < n_it:
                    stage3a(t - 2)
                if 0 <= t - 1 < n_it:
                    stage2b(t - 1)
                if 0 <= t - 2 < n_it:
                    stage3b(t - 2)
            while cast_jobs:
                issue_cast()
            S.barrier()

        with ExitStack() as pc:
            pb0 = _alloc(nc, pc, "bC0", [128, 512], F32, n=1, psum=True)[0][0]
            pb1 = _alloc(nc, pc, "bC1", [128, 512], F32, n=2, psum=True)
            tpA = _alloc(nc, pc, "bCtA", [128, 1024], BF16, n=1, psum=True)[0][0]
            tpB = _alloc(nc, pc, "bCtB", [128, 1024], BF16, n=1, psum=True)[0][0]
            sTb = Rot([(pb0[:, i * 128:(i + 1) * 128], T()) for i in range(4)])
            obk = Rot([(pb1[j][0][:, i * 256:(i + 1) * 256], T()) for i in range(2) for j in range(2)])
            kv_banks = _alloc(nc, pc, "bkv", [128, 512], F32, n=3, psum=True)
            kvb = Rot(kv_banks)
            tpr_l = [(tpA[:, i * 256:(i + 1) * 256], T()) for i in range(2)]
            tpo_t = T()
            tpo_l = [(tpB[:, i * 256:(i + 1) * 256], tpo_t) for i in range(2)]
            tpr = Rot(tpr_l)
            tpo = Rot(tpo_l)
            ptp = [tpr_l[0], tpo_l[0]]
            rmk = _alloc(nc, pc, "rmk_s", [128, NRT * 128], F32)[0]
            rvc = _alloc(nc, pc, "rvc_s", [128, 13 * NRT], F32)[0]
            S.dma("sp", rmk[0][:, :], rmask[:, :], writes=[rmk[1]])
            S.dma("sp", rvc[0][:, :], rvec[:, :], writes=[rvc[1]])
            CG = 8
            HI = min(4, NRT)
            ngrp = nch // CG
            qs = Rot(_alloc(nc, pc, "rq", [128, 2, CG * 128], BF16, n=HI))
            ks = Rot(_alloc(nc, pc, "rk", [128, 2, CG * 128], BF16, n=2 * HI))
            vs = Rot(_alloc(nc, pc, "rv", [128, CG, 256], BF16, n=2 * HI))
            gs = Rot(_alloc(nc, pc, "rg", [128, CG, 256], F32, n=HI))
            oTs = Rot(_alloc(nc, pc, "roT", [128, 2, CG * 128], BF16, n=HI))
            obfs = Rot(_alloc(nc, pc, "obf", [128, 256], BF16, n=8))
            Tst = [_alloc(nc, pc, "Tst%d" % h, [128, 512], F32)[0] for h in range(HI)]
            Sbf = [Rot(_alloc(nc, pc, "Sbf%d_" % h, [128, 512], BF16, n=2)) for h in range(HI)]
            PTs = Rot(_alloc(nc, pc, "PT", [128, 128], BF16, n=6))
            Kps = Rot(_alloc(nc, pc, "Kp", [128, 256], BF16, n=6))
            ys = Rot(_alloc(nc, pc, "y", [128, 256], F32, n=8))
            y2s = Rot(_alloc(nc, pc, "y2", [128, 256], F32, n=8))
            smalls = Rot(_alloc(nc, pc, "sm", [128, 4], F32, n=12))
            RVv = RV.rearrange("(n p) d -> p n d", p=128)
            Gv = G.rearrange("(n p) d -> p n d", p=128)
            for pair in range(NRT // HI):
                cur_S = [None] * HI
                for h in range(HI):
                    S.op("pool", lambda e, h=h: e.memset(Tst[h][0][:, :], 0.0), writes=[Tst[h][1]])
                for gi in range(ngrp):
                    t0 = gi * CG * 128
                    own = t0 >= NP
                    to = t0 - NP
                    ld = []
                    for h in range(HI):
                        hg_ = pair * HI + h
                        k, k_t = ks.next(); v, v_t = vs.next()
                        q = q_t = gg = gg_t = oT = oT_t = None
                        if own:
                            q, q_t = qs.next(); gg, gg_t = gs.next(); oT, oT_t = oTs.next()
                        for half in range(2):
                            r0 = (hg_ * 2 + half) * 128
                            S.dma("sp", k[:, half, :], RKT[r0:r0 + 128, t0:t0 + CG * 128], reads=[t_RKT], writes=[k_t], partial=(half > 0))
                            if own:
                                S.dma("sp", q[:, half, :], RQT[r0:r0 + 128, to:to + CG * 128], reads=[t_RQT], writes=[q_t], partial=(half > 0))
                        S.dma("sp", v[:, :, :], RVv[:, gi * CG:(gi + 1) * CG, hg_ * 256:(hg_ + 1) * 256], reads=[t_RV], writes=[v_t])
                        if own:
                            go = to // 128
                            S.dma("sp", gg[:, :, :], Gv[:, go:go + CG, hg_ * 256:(hg_ + 1) * 256], reads=[t_G], writes=[gg_t])
                        ld.append(((q, q_t), (k, k_t), (v, v_t), (gg, gg_t), (oT, oT_t)))
                    if not own:
                        items = [(h, j) for h in range(HI) for j in range(CG)]
                        pst = {}

                        def emitA(idx):
                            h, j = items[idx]
                            hg_ = pair * HI + h
                            (q, q_t), (k, k_t), (v, v_t), (gg, gg_t), (oT, oT_t) = ld[h]
                            tp, tp_t = ptp[idx % 2]
                            for half in range(2):
                                S.op("pe", lambda e, half=half: e.transpose(
                                    tp[:, half * 128:(half + 1) * 128], k[:, half, j * 128:(j + 1) * 128], ident[0]),
                                    reads=[k_t, ident[1]], writes=[tp_t], partial=(half > 0))
                            Kp, Kp_t = Kps.next()
                            col = 5 * NRT + hg_ * 8 + j
                            if idx % 2 == 0:
                                S.op("act", lambda e: e.activation(out=Kp[:, :], in_=tp, func=AF.Copy, scale=rvc[0][:, col:col + 1]),
                                     reads=[tp_t, rvc[1]], writes=[Kp_t])
                            else:
                                S.op("dve", lambda e: e.tensor_scalar(out=Kp[:, :], in0=tp, scalar1=rvc[0][:, col:col + 1], scalar2=None, op0=ALU.mult),
                                     reads=[tp_t, rvc[1]], writes=[Kp_t])
                            pst[idx] = (Kp, Kp_t)

                        def emitB(idx):
                            h, j = items[idx]
                            hg_ = pair * HI + h
                            (q, q_t), (k, k_t), (v, v_t), (gg, gg_t), (oT, oT_t) = ld[h]
                            kv, kv_t = kv_banks[h % 2]
                            Kp, Kp_t = pst.pop(idx)
                            if j == 0:
                                S.op("pe", lambda e: e.matmul(kv[:, :], lhsT=zerosb[0], rhs=k[:, 0, 0:512], start=True, stop=False,
                                                              skip_group_check=True),
                                     reads=[k_t, zerosb[1]], writes=[kv_t])
                            for half in range(2):
                                S.op("pe", lambda e, half=half: e.matmul(
                                    kv[:, half * 256:(half + 1) * 256], lhsT=Kp[:, half * 128:(half + 1) * 128], rhs=v[:, j, :],
                                    start=False, stop=(j == CG - 1), skip_group_check=True),
                                    reads=[Kp_t, v_t], writes=[kv_t], partial=True)
                            if j == CG - 1:
                                Tt, Tt_t = Tst[h]
                                gcol = 3 * NRT + hg_
                                S.op("dve", lambda e: e.scalar_tensor_tensor(
                                    out=Tt[:, :], in0=Tt[:, :], scalar=rvc[0][:, gcol:gcol + 1], in1=kv[:, :], op0=ALU.mult, op1=ALU.add),
                                    reads=[Tt_t, kv_t, rvc[1]], writes=[Tt_t])

                        for idx in range(len(items) + 1):
                            if idx < len(items):
                                emitA(idx)
                            if idx >= 1:
                                emitB(idx - 1)
                        continue
                    if gi == npch // CG and npch > 0:
                        for h in range(HI):
                            hg_ = pair * HI + h
                            Tt, Tt_t = Tst[h]
                            Sn, Sn_t = Sbf[h].next()
                            S.op("act", lambda e, Sn=Sn, Tt=Tt: e.activation(out=Sn[:, :], in_=Tt[:, :], func=AF.Copy),
                                 reads=[Tt_t], writes=[Sn_t])
                            cur_S[h] = (Sn, Sn_t)
                            icol = 4 * NRT + hg_
                            S.op("dve", lambda e, Tt=Tt, icol=icol: e.tensor_scalar(
                                out=Tt[:, :], in0=Tt[:, :], scalar1=rvc[0][:, icol:icol + 1], scalar2=None, op0=ALU.mult),
                                reads=[Tt_t, rvc[1]], writes=[Tt_t])
                    for c in range(CG):
                        ch = gi * CG + c
                        cs_ = slice(c * 128, (c + 1) * 128)
                        upd = ch < nch - 1
                        st = [dict() for _ in range(HI)]
                        for h in range(HI):
                            hg_ = pair * HI + h
                            (q, q_t), (k, k_t), (v, v_t), (gg, gg_t), (oT, oT_t) = ld[h]
                            sT, sT_t = sTb.next()
                            for half in range(2):
                                S.op("pe", lambda e, half=half, sT=sT, k=k, q=q: e.matmul(
                                    sT[:, :], lhsT=k[:, half, cs_], rhs=q[:, half, cs_], start=(half == 0), stop=(half == 1)),
                                    reads=[k_t, q_t], writes=[sT_t], partial=(half > 0))
                            PT, PT_t = PTs.next()
                            S.op("dve", lambda e, PT=PT, sT=sT, hg_=hg_: e.tensor_tensor(
                                out=PT[:, :], in0=sT[:, :], in1=rmk[0][:, hg_ * 128:(hg_ + 1) * 128], op=ALU.mult),
                                reads=[sT_t, rmk[1]], writes=[PT_t])
                            O, O_t = obk.next()
                            guard = [st[h - 2]["y"][1]] if h >= 2 else []
                            S.op("pe", lambda e, O=O, PT=PT, v=v: e.matmul(
                                O[:, :], lhsT=PT[:, :], rhs=v[:, c, :], start=True, stop=(ch == 0)),
                                reads=[PT_t, v_t] + guard, writes=[O_t])
                            if ch > 0:
                                Sb, Sb_t = cur_S[h]
                                for half in range(2):
                                    S.op("pe", lambda e, O=O, q=q, Sb=Sb, half=half: e.matmul(
                                        O[:, :], lhsT=q[:, half, cs_], rhs=Sb[:, half * 256:(half + 1) * 256], start=False, stop=(half == 1)),
                                        reads=[q_t, Sb_t], writes=[O_t], partial=True)
                            if upd:
                                tp, tp_t = tpr.next()
                                for half in range(2):
                                    S.op("pe", lambda e, tp=tp, k=k, half=half: e.transpose(
                                        tp[:, half * 128:(half + 1) * 128], k[:, half, cs_], ident[0]),
                                        reads=[k_t, ident[1]], writes=[tp_t], partial=(half > 0))
                                Kp, Kp_t = Kps.next()
                                S.op("act", lambda e, Kp=Kp, tp=tp, hg_=hg_: e.activation(
                                    out=Kp[:, :], in_=tp[:, :], func=AF.Copy, scale=rvc[0][:, hg_:hg_ + 1]),
                                    reads=[tp_t, rvc[1]], writes=[Kp_t])
                                kv, kv_t = kvb.next()
                                for half in range(2):
                                    S.op("pe", lambda e, kv=kv, Kp=Kp, v=v, half=half: e.matmul(
                                        kv[:, half * 256:(half + 1) * 256], lhsT=Kp[:, half * 128:(half + 1) * 128], rhs=v[:, c, :],
                                        start=True, stop=True),
                                        reads=[Kp_t, v_t], writes=[kv_t], partial=(half > 0))
                            y, y_t = ys.next()
                            st[h]["y"] = (y, y_t)
                            S.op("act", lambda e, y=y, O=O, hg_=hg_: e.activation(
                                out=y[:, :], in_=O[:, :], func=AF.Copy, scale=rvc[0][:, NRT + hg_:NRT + hg_ + 1]),
                                reads=[O_t, rvc[1]], writes=[y_t])
                            if upd:
                                Tt, Tt_t = Tst[h]
                                gcol = 2 * NRT + hg_
                                S.op("dve", lambda e, Tt=Tt, kv=kv, gcol=gcol: e.scalar_tensor_tensor(
                                    out=Tt[:, :], in0=Tt[:, :], scalar=rvc[0][:, gcol:gcol + 1], in1=kv[:, :], op0=ALU.mult, op1=ALU.add),
                                    reads=[Tt_t, kv_t, rvc[1]], writes=[Tt_t])
                                Sn, Sn_t = Sbf[h].next()
                                S.op("act", lambda e, Sn=Sn, Tt=Tt, gcol=gcol: e.activation(
                                    out=Sn[:, :], in_=Tt[:, :], func=AF.Copy, scale=rvc[0][:, gcol:gcol + 1]),
                                    reads=[Tt_t, rvc[1]], writes=[Sn_t])
                                cur_S[h] = (Sn, Sn_t)
                        for h in range(HI):
                            y, y_t = st[h]["y"]
                            st[h]["sm"] = smalls.next()
                            sm, sm_t = st[h]["sm"]
                            S.op("dve", lambda e, sm=sm, y=y: e.tensor_reduce(out=sm[:, 0:1], in_=y[:, :], axis=AX.X, op=ALU.add),
                                 reads=[y_t], writes=[sm_t])
                            S.op("dve", lambda e, sm=sm: e.tensor_scalar(out=sm[:, 1:2], in0=sm[:, 0:1], scalar1=-1.0 / 256.0, scalar2=None, op0=ALU.mult),
                                 reads=[sm_t], writes=[sm_t])
                        for h in range(HI):
                            y, y_t = st[h]["y"]; sm, sm_t = st[h]["sm"]
                            st[h]["y2"] = y2s.next()
                            y2, y2_t = st[h]["y2"]
                            S.op("act", lambda e, y2=y2, y=y, sm=sm: e.activation(out=y2[:, :], in_=y[:, :], func=AF.Square, bias=sm[:, 1:2], scale=1.0),
                                 reads=[y_t, sm_t], writes=[y2_t])
                        for h in range(HI):
                            y2, y2_t = st[h]["y2"]; sm, sm_t = st[h]["sm"]
                            S.op("dve", lambda e, sm=sm, y2=y2: e.tensor_reduce(out=sm[:, 2:3], in_=y2[:, :], axis=AX.X, op=ALU.add),
                                 reads=[y2_t], writes=[sm_t])
                        for h in range(HI):
                            sm, sm_t = st[h]["sm"]
                            S.op("act", lambda e, sm=sm: e.activation(out=sm[:, 3:4], in_=sm[:, 2:3], func=AF.Sqrt, bias=eps_t[0][:, 0:1], scale=1.0 / 256.0),
                                 reads=[sm_t, eps_t[1]], writes=[sm_t])
                        for h in range(HI):
                            gg, gg_t = ld[h][3]
                            y, y_t = st[h]["y"]; sm, sm_t = st[h]["sm"]
                            S.op("dve", lambda e, sm=sm: e.reciprocal(out=sm[:, 3:4], in_=sm[:, 3:4]), reads=[sm_t], writes=[sm_t])
                            S.op("dve", lambda e, y=y, sm=sm: e.tensor_scalar(out=y[:, :], in0=y[:, :], scalar1=sm[:, 1:2], scalar2=sm[:, 3:4], op0=ALU.add, op1=ALU.mult),
                                 reads=[y_t, sm_t], writes=[y_t])
                            st[h]["ob"] = obfs.next()
                            ob_, ob_t = st[h]["ob"]
                            S.op("dve", lambda e, ob_=ob_, y=y, gg=gg: e.tensor_tensor(out=ob_[:, :], in0=y[:, :], in1=gg[:, c, :], op=ALU.mult),
                                 reads=[y_t, gg_t], writes=[ob_t])
                        for h in range(HI):
                            oT, oT_t = ld[h][4]
                            ob_, ob_t = st[h]["ob"]
                            to_, to_t = tpo.next()
                            for half in range(2):
                                S.op("pe", lambda e, to_=to_, ob_=ob_, half=half: e.transpose(
                                    to_[:, half * 128:(half + 1) * 128], ob_[:, half * 128:(half + 1) * 128], ident[0]),
                                    reads=[ob_t, ident[1]], writes=[to_t], partial=(half > 0))
                            S.op("act", lambda e, oT=oT, to_=to_: e.activation(
                                out=oT[:, :, cs_], in_=to_[:, :].rearrange("p (a b) -> p a b", a=2), func=AF.Copy),
                                reads=[to_t], writes=[oT_t], partial=(c > 0))
                    if own:
                        for h in range(HI):
                            hg_ = pair * HI + h
                            oT, oT_t = ld[h][4]
                            for half in range(2):
                                r0 = 2048 + hg_ * 256 + half * 128
                                S.dma("sp", mixT[r0:r0 + 128, to:to + CG * 128], oT[:, half, :],
                                      reads=[oT_t], writes=[t_mix], partial=True)
            S.barrier()

        if do_ffn:
            with ExitStack() as pd:
                nblk = NO // TB
                gv = _alloc(nc, pd, "gv", [128, 80], F32)[0]
                S.dma("sp", gv[0][:, :], gvec[:, :], writes=[gv[1]])
                banks = _alloc(nc, pd, "bankD", [128, 512], F32, n=8, psum=True)
                mmb = Rot(banks[0:6])
                stb = Rot(banks[6:8])
                x2q = _alloc(nc, pd, "x2", [128, 8, TB], F32, n=4)
                act = _alloc(nc, pd, "act", [128, 32, TB], BF16)[0]
                us = Rot(_alloc(nc, pd, "u", [128, 16, TB], BF16, n=2))
                pans = Rot(_alloc(nc, pd, "panD", [128, 8192], BF16, n=3))
                sqrot = Rot(_alloc(nc, pd, "sqD", [128, TB], BF16, n=2))
                rts = Rot(_alloc(nc, pd, "rtD", [128, TB], F32, n=2))
                rls = Rot(_alloc(nc, pd, "rl", [128, TB], F32, n=3))
                xTv = xT.rearrange("(k p) t -> p k t", p=128)
                mTv = mixT.rearrange("(k p) t -> p k t", p=128)
                oTv = outT.rearrange("(k p) t -> p k t", p=128)
                aa, a_t = act

                def X2(cc):
                    t, tr = x2q[cc // 8]
                    return t[:, cc % 8, :], tr

                def stats(chunks, src_t, nfeat):
                    bank, bank_t = stb.next()
                    rt, rt_t = rts.next()
                    _stats(S, ones, chunks, src_t, sqrot, bank, bank_t, rt, rt_t, eps_t, 1.0 / nfeat, TB)
                    return rt, rt_t

                def prologue_load(tb):
                    t0 = tb * TB
                    S.dma("act", aa[:, :, :], mTv[:, :, t0:t0 + TB], reads=[t_mix], writes=[a_t])

                def prologue_norm(tb):
                    rt, rt_t = stats([aa[:, k, :] for k in range(16)], a_t, 2048.0)
                    for k in range(16):
                        S.op("dve", lambda e, k=k: e.scalar_tensor_tensor(
                            out=aa[:, k, :], in0=aa[:, k, :], scalar=gv[0][:, k:k + 1], in1=rt[:, :], op0=ALU.mult, op1=ALU.mult),
                            reads=[a_t, gv[1], rt_t], writes=[a_t])

                def load_x(tb):
                    t0 = tb * TB
                    for qd in range(4):
                        S.dma("act", x2q[qd][0][:, :, :], xTv[:, qd * 8:(qd + 1) * 8, NP + t0:NP + t0 + TB], writes=[x2q[qd][1]])

                load_x(0)
                prologue_load(0)
                prologue_norm(0)
                for tb in range(nblk):
                    t0 = tb * TB
                    for pn in range(16):
                        pan, pan_t = pans.next()
                        S.dma("sp", pan[:, :], wo_b[pn * 128:(pn + 1) * 128, :], reads=[t_wo[pn]], writes=[pan_t])
                        pv = pan[:, :].rearrange("p (k c) -> p k c", k=32)
                        for ct in range(2):
                            bk, bk_t = mmb.next()
                            for k in range(32):
                                S.op("pe", lambda e, bk=bk, k=k, ct=ct, pv=pv: e.matmul(
                                    bk[:, :], lhsT=pv[:, k, ct * 128:(ct + 1) * 128], rhs=aa[:, k, :], start=(k == 0), stop=(k == 31)),
                                    reads=[pan_t, a_t], writes=[bk_t], partial=(k > 0))
                            xc, xc_t = X2(pn * 2 + ct)
                            S.op("dve", lambda e, bk=bk, xc=xc: e.tensor_tensor(out=xc, in0=bk[:, :], in1=xc, op=ALU.add),
                                 reads=[bk_t, xc_t], writes=[xc_t], partial=True)
                    rt, rt_t = stats([X2(k)[0] for k in range(32)], [X2(k)[1] for k in range(32)], float(D))
                    for k in range(32):
                        xc, xc_t = X2(k)
                        S.op("dve", lambda e, k=k, xc=xc: e.scalar_tensor_tensor(
                            out=aa[:, k, :], in0=xc, scalar=gv[0][:, 16 + k:17 + k], in1=rt[:, :], op0=ALU.mult, op1=ALU.mult),
                            reads=[xc_t, gv[1], rt_t], writes=[a_t], partial=(k > 0))
                    for fb in range(8):
                        u, u_t = us.next()
                        for i in range(8):
                            pn = fb * 8 + i
                            pan, pan_t = pans.next()
                            S.dma("sp", pan[:, :], wu_b[pn * 128:(pn + 1) * 128, :], reads=[t_wu[pn]], writes=[pan_t])
                            pv = pan[:, :].rearrange("p (k c) -> p k c", k=32)
                            for ft in range(2):
                                bk, bk_t = mmb.next()
                                for k in range(32):
                                    S.op("pe", lambda e, bk=bk, k=k, ft=ft, pv=pv: e.matmul(
                                        bk[:, :], lhsT=pv[:, k, ft * 128:(ft + 1) * 128], rhs=aa[:, k, :], start=(k == 0), stop=(k == 31)),
                                        reads=[pan_t, a_t], writes=[bk_t], partial=(k > 0))
                                rl, rl_t = rls.next()
                                S.op("act", lambda e, rl=rl, bk=bk: e.activation(out=rl[:, :], in_=bk[:, :], func=AF.Relu),
                                     reads=[bk_t], writes=[rl_t])
                                fc = i * 2 + ft
                                S.op("pool", lambda e, rl=rl, fc=fc, u=u: e.tensor_tensor(out=u[:, fc, :], in0=rl[:, :], in1=rl[:, :], op=ALU.mult),
                                     reads=[rl_t], writes=[u_t], partial=(fc > 0))
                        if fb == 7 and tb + 1 < nblk:
                            prologue_load(tb + 1)
                        for cg in range(8):
                            if fb == 7 and tb + 1 < nblk and cg == 5:
                                prologue_norm(tb + 1)
                            pn = fb * 8 + cg
                            pan, pan_t = pans.next()
                            S.dma("sp", pan[:, :], wd_b[pn * 128:(pn + 1) * 128, :], reads=[t_wd[pn]], writes=[pan_t])
                            pv = pan[:, :].rearrange("p (k c) -> p k c", k=16)
                            for ct in range(4):
                                bk, bk_t = mmb.next()
                                for fk in range(16):
                                    S.op("pe", lambda e, bk=bk, fk=fk, ct=ct, pv=pv, u=u: e.matmul(
                                        bk[:, :], lhsT=pv[:, fk, ct * 128:(ct + 1) * 128], rhs=u[:, fk, :], start=(fk == 0), stop=(fk == 15)),
                                        reads=[pan_t, u_t], writes=[bk_t], partial=(fk > 0))
                                xc, xc_t = X2(cg * 4 + ct)
                                S.op("dve", lambda e, bk=bk, xc=xc: e.tensor_tensor(out=xc, in0=bk[:, :], in1=xc, op=ALU.add),
                                     reads=[bk_t, xc_t], writes=[xc_t], partial=True)
                    rt, rt_t = stats([X2(k)[0] for k in range(32)], [X2(k)[1] for k in range(32)], float(D))
                    for k in range(32):
                        xc, xc_t = X2(k)
                        S.op("dve", lambda e, k=k, xc=xc: e.scalar_tensor_tensor(
                            out=xc, in0=xc, scalar=gv[0][:, 48 + k:49 + k], in1=rt[:, :], op0=ALU.mult, op1=ALU.mult),
                            reads=[xc_t, gv[1], rt_t], writes=[xc_t], partial=True)
                        if k % 8 == 7:
                            qd = k // 8
                            S.dma("act", oTv[:, qd * 8:(qd + 1) * 8, t0:t0 + TB], x2q[qd][0][:, :, :],
                                  reads=[x2q[qd][1]], writes=[t_out], partial=True)
                    if tb + 1 < nblk:
                        load_x(tb + 1)
                S.barrier()
    return nc


def _panels(w, ncols, nk):
    K, C = w.shape
    assert K == nk * 128
    npan = C // ncols
    v = w.reshape(nk, 128, npan, ncols).transpose(2, 1, 0, 3)
    return np.ascontiguousarray(v).reshape(npan * 128, nk * ncols)


def _consts(NRT, pos):
    bf = ml_dtypes.bfloat16
    idx = np.arange(128)
    ones = np.ones((128, 128), np.float32)
    ident = np.eye(128, dtype=np.float32)
    negU = -(idx[:, None] >= idx[None, :]).astype(np.float32)
    cbf = np.concatenate([ones, ident, negU, -ones, 0 * ones], axis=1).astype(bf)
    mask01 = (idx[:, None] < idx[None, :]).astype(np.float32)
    rmask = np.zeros((128, NRT * 128), np.float32)
    rvec = np.zeros((128, 13 * NRT), np.float32)
    m = idx.astype(np.float64)
    for h in range(NRT):
        gamma = 1.0 - 2.0 ** (-5.0 - h)
        ks = gamma ** (-(m + 1.0))
        rmask[:, h * 128:(h + 1) * 128] = (ks[:, None] * (idx[None, :] >= idx[:, None])).astype(np.float32)
        rvec[:, h] = ks
        rvec[:, NRT + h] = gamma ** (m + 1.0)
        rvec[:, 2 * NRT + h] = gamma ** 128.0
        rvec[:, 3 * NRT + h] = gamma ** 1024.0
        rvec[:, 4 * NRT + h] = gamma ** -128.0
        for j in range(8):
            rvec[:, 5 * NRT + h * 8 + j] = gamma ** (1023.0 - (128.0 * j + m))
    inv_freq = np.power(np.float32(10000.0), -np.arange(128, dtype=np.float32) / np.float32(128.0)).astype(np.float32)
    ang = (pos.astype(np.float32)[:, None] * inv_freq[None, :]).astype(np.float32)
    cosT = np.ascontiguousarray(np.cos(ang.astype(np.float64)).astype(np.float32).T)
    sinT = np.ascontiguousarray(np.sin(ang.astype(np.float64)).astype(np.float32).T)
    return dict(cbf=cbf, mask01=mask01, rmask=rmask, rvec=rvec, cosT=cosT, sinT=sinT)


def _vec128(g, nk):
    return np.ascontiguousarray(np.asarray(g, np.float32).reshape(nk, 128).T)


_NC_CACHE = {}
_OFFS = (0, 2048, 4096, 6144, 8192, 10240, 12288)


def kernel(x, attn_norm_g, w_in, sb_norm_g, ret_norm_g, w_out, mlp_norm_g, w_up, w_down, final_norm_g):
    x = np.asarray(x, np.float32)
    B, S_len, _ = x.shape
    NO = S_len // 4
    NP = S_len - NO
    if "nc" not in _NC_CACHE:
        _NC_CACHE["nc"] = build_fused(NP, NO, 4, True, False)
    nc = _NC_CACHE["nc"]
    w_in0 = np.asarray(w_in, np.float32)[0]
    win_p = np.concatenate([_panels(w_in0[:, o + 512 * hg:o + 512 * hg + 512], 512, 32)
                            for o in _OFFS for hg in range(4)], axis=0)
    wo_p = _panels(np.asarray(w_out, np.float32)[0], 256, 32)
    wu_p = _panels(np.asarray(w_up, np.float32)[0], 256, 32)
    wd = np.asarray(w_down, np.float32)[0]
    wd_p = np.concatenate([_panels(wd[fb * 2048:(fb + 1) * 2048], 512, 16) for fb in range(8)], axis=0)
    gvec = np.concatenate([_vec128(np.asarray(sb_norm_g)[0], 16), _vec128(np.asarray(mlp_norm_g)[0], 32),
                           _vec128(np.asarray(final_norm_g), 32)], axis=1)
    g_attn = _vec128(np.asarray(attn_norm_g)[0], 32)
    gret = np.ascontiguousarray(np.broadcast_to(np.asarray(ret_norm_g, np.float32)[0][None, :], (128, 2048)))
    in_maps = []
    for c in range(8):
        b, q = c // 4, c % 4
        start = q * NO - NP
        xw = np.zeros((D, NP + NO), np.float32)
        lo = max(0, start)
        xw[:, lo - start:] = x[b, lo:q * NO + NO, :].T
        pos = np.arange(start, start + NP + NO)
        kbias = np.zeros((128, (NP + NO) // 128), np.float32)
        kbias[:, :(lo - start) // 128] = -200.0
        m = dict(xT=xw, win_p=win_p, g_attn=g_attn, gret=gret, kbias=kbias,
                 wo_p=wo_p, wu_p=wu_p, wd_p=wd_p, gvec=gvec)
        m.update(_consts(8, pos))
        in_maps.append(m)
    res = run_bass_kernel_spmd(nc, in_maps, core_ids=list(range(8))).results
    out = np.empty((B, S_len, D), np.float32)
    for c in range(8):
        b, q = c // 4, c % 4
        out[b, q * NO:(q + 1) * NO, :] = np.asarray(res[c]["outT"]).T
    return out
```
